# Optimizing a Trainium2 kernel written in Bass

```python
import math
import jax
import jax.numpy as jnp
from jax import lax
import numpy as np

D_MODEL = 2048
BATCH = 4
SEQ = 4096
DEPTH = 2

GRID_W = 64
CTX_LEN = 256
HEAD_DIM = 128
N_GROUPS = 4
HEADS_PER_GROUP = D_MODEL // (N_GROUPS * HEAD_DIM)
D_MIX = N_GROUPS * HEADS_PER_GROUP * HEAD_DIM
Q_BLOCK = 128
ROPE_THETA = 10000.0
EPS = 1e-6
NA_KH_MAX = 8
NA_KW = 16
GQA_KV_HEADS = 2
MLA_Q_RANK = 512
MLA_KV_RANK = 256
MLA_NOPE = 128
MLA_ROPE = 64
MLA_V = 128
DIFF_D = 64
D_FF = -(-8 * D_MODEL // (3 * 256)) * 256

IN_SPLITS = (
    HEADS_PER_GROUP * HEAD_DIM, HEADS_PER_GROUP * HEAD_DIM, HEADS_PER_GROUP * HEAD_DIM,
    HEADS_PER_GROUP * HEAD_DIM, GQA_KV_HEADS * HEAD_DIM, GQA_KV_HEADS * HEAD_DIM,
    MLA_Q_RANK, MLA_KV_RANK, MLA_ROPE,
    HEADS_PER_GROUP * 2 * DIFF_D, HEADS_PER_GROUP * 2 * DIFF_D, HEADS_PER_GROUP * 2 * DIFF_D,
)
D_IN = sum(IN_SPLITS)

kernel_name = 'hybrid_dit_parallel_head_groups'


def rms_norm(x, g):
    xf = x.astype(jnp.float32)
    y = xf * lax.rsqrt(jnp.mean(xf * xf, axis=-1, keepdims=True) + EPS)
    return y.astype(x.dtype) * g


def axial_rope(n_tokens, dim):
    t = jnp.arange(n_tokens, dtype=jnp.int32)
    row = (t // GRID_W).astype(jnp.float32)
    col = (t % GRID_W).astype(jnp.float32)
    d_axis = dim // 2
    inv = 1.0 / (ROPE_THETA ** (jnp.arange(0, d_axis, 2, dtype=jnp.float32) / d_axis))
    ang = jnp.concatenate([row[:, None] * inv, col[:, None] * inv], axis=-1)
    return jnp.cos(ang), jnp.sin(ang)


def apply_rope(x, cos, sin):
    x1, x2 = jnp.split(x, 2, axis=-1)
    return jnp.concatenate([x1 * cos - x2 * sin, x1 * sin + x2 * cos], axis=-1).astype(x.dtype)


def to_heads(a, n_heads):
    b, t, _ = a.shape
    return a.reshape(b, t, n_heads, -1).transpose(0, 2, 1, 3)


def over_query_blocks(fn, q):
    *lead, s, d = q.shape
    nb = s // Q_BLOCK
    qb = jnp.moveaxis(q.reshape(*lead, nb, Q_BLOCK, d), -3, 0)
    out = lax.map(fn, qb)
    out = jnp.moveaxis(out, 0, -3)
    return out.reshape(*out.shape[:-3], s, out.shape[-1])


def softmax_attend(q, k, v, scale):
    s = jnp.einsum('bkgqd,bktd->bkgqt', q, k).astype(jnp.float32) * scale
    p = jax.nn.softmax(s, axis=-1).astype(v.dtype)
    return jnp.einsum('bkgqt,bktd->bkgqd', p, v)


def neighbourhood_attention(q, k, v, k_ctx, v_ctx, rpb):
    b, h, s, d = q.shape
    rows = s // GRID_W
    kh = min(NA_KH_MAX, rows)
    scale = d ** -0.5
    qg = q.reshape(b, h, rows, GRID_W, d)
    kg = k.reshape(b, h, rows, GRID_W, d)
    vg = v.reshape(b, h, rows, GRID_W, d)
    cols = np.arange(GRID_W)
    c_start = np.clip(cols - NA_KW // 2, 0, GRID_W - NA_KW)
    col_idx = c_start[:, None] + np.arange(NA_KW)[None, :]
    dc_idx = col_idx - cols[:, None] + (NA_KW - 1)
    col_bias = rpb[:, :, dc_idx]
    n_win = kh * NA_KW

    def row_block(r):
        r_start = jnp.clip(r - kh // 2, 0, rows - kh)
        q_r = lax.dynamic_index_in_dim(qg, r, axis=2, keepdims=False)
        k_win = lax.dynamic_slice_in_dim(kg, r_start, kh, axis=2)[:, :, :, col_idx]
        v_win = lax.dynamic_slice_in_dim(vg, r_start, kh, axis=2)[:, :, :, col_idx]
        dr_idx = r_start + jnp.arange(kh) - r + (NA_KH_MAX - 1)
        bias = jnp.take(col_bias, dr_idx, axis=1).transpose(0, 2, 1, 3)
        s_win = jnp.einsum('bhcd,bhicjd->bhcij', q_r, k_win).astype(jnp.float32) * scale + bias.astype(jnp.float32)
        s_ctx = jnp.einsum('bhcd,bhtd->bhct', q_r, k_ctx).astype(jnp.float32) * scale
        p = jax.nn.softmax(jnp.concatenate([s_win.reshape(b, h, GRID_W, n_win), s_ctx], axis=-1), axis=-1).astype(v.dtype)
        p_win = p[..., :n_win].reshape(b, h, GRID_W, kh, NA_KW)
        p_ctx = p[..., n_win:]
        return jnp.einsum('bhcij,bhicjd->bhcd', p_win, v_win) + jnp.einsum('bhct,bhtd->bhcd', p_ctx, v_ctx)

    out = lax.map(row_block, jnp.arange(rows))
    return jnp.moveaxis(out, 0, 2).reshape(b, h, s, d)


def na_mixer(q_l, k_l, v_l, q_c, k_c, v_c, rpb, ctx_out):
    H = HEADS_PER_GROUP
    kc, vc = to_heads(k_c, H), to_heads(v_c, H)
    ol = neighbourhood_attention(to_heads(q_l, H), to_heads(k_l, H), to_heads(v_l, H), kc, vc, rpb)
    oc = None
    if ctx_out:
        oc = softmax_attend(to_heads(q_c, H)[:, :, None], kc, vc, HEAD_DIM ** -0.5)[:, :, 0]
    return ol, oc


def gqa_mixer(q_l, k_l, v_l, q_c, k_c, v_c, g_q, g_k, rope, ctx_out):
    H, KV = HEADS_PER_GROUP, GQA_KV_HEADS
    G = H // KV
    cos, sin = rope
    scale = HEAD_DIM ** -0.5

    def grouped(q):
        b, _, t, d = q.shape
        return q.reshape(b, KV, G, t, d)

    ql = grouped(apply_rope(rms_norm(to_heads(q_l, H), g_q), cos, sin))
    kl = apply_rope(rms_norm(to_heads(k_l, KV), g_k), cos, sin)
    kc = rms_norm(to_heads(k_c, KV), g_k)
    vc = to_heads(v_c, KV)
    k_all = jnp.concatenate([kc, kl], axis=2)
    v_all = jnp.concatenate([vc, to_heads(v_l, KV)], axis=2)
    ol = over_query_blocks(lambda qb: softmax_attend(qb, k_all, v_all, scale), ql)
    b, _, _, s, d = ol.shape
    ol = ol.reshape(b, H, s, d)
    oc = None
    if ctx_out:
        qc = grouped(rms_norm(to_heads(q_c, H), g_q))
        oc = softmax_attend(qc, kc, vc, scale).reshape(b, H, -1, d)
    return ol, oc


def mla_mixer(cq_l, ckv_l, kpe_l, cq_c, ckv_c, kpe_c, g_q, g_kv, w_uq, w_ukv, rope, ctx_out):
    H = HEADS_PER_GROUP
    cos, sin = rope
    scale = (MLA_NOPE + MLA_ROPE) ** -0.5

    def queries(cq, rot):
        q = to_heads(rms_norm(cq, g_q) @ w_uq, H)
        q_nope, q_pe = q[..., :MLA_NOPE], q[..., MLA_NOPE:]
        if rot:
            q_pe = apply_rope(q_pe, cos, sin)
        return jnp.concatenate([q_nope, q_pe], axis=-1)[:, :, None]

    def keys_values(ckv, kpe, rot):
        kv = to_heads(rms_norm(ckv, g_kv) @ w_ukv, H)
        k_nope, v = kv[..., :MLA_NOPE], kv[..., MLA_NOPE:]
        k_pe = kpe[:, None]
        if rot:
            k_pe = apply_rope(k_pe, cos, sin)
        k_pe = jnp.broadcast_to(k_pe, k_nope.shape[:-1] + (MLA_ROPE,))
        return jnp.concatenate([k_nope, k_pe], axis=-1), v

    kl, vl = keys_values(ckv_l, kpe_l, True)
    kc, vc = keys_values(ckv_c, kpe_c, False)
    k_all = jnp.concatenate([kc, kl], axis=2)
    v_all = jnp.concatenate([vc, vl], axis=2)
    ol = over_query_blocks(lambda qb: softmax_attend(qb, k_all, v_all, scale), queries(cq_l, True))[:, :, 0]
    oc = None
    if ctx_out:
        oc = softmax_attend(queries(cq_c, False), kc, vc, scale)[:, :, 0]
    return ol, oc


def diff_mixer(q_l, k_l, v_l, q_c, k_c, v_c, lq1, lk1, lq2, lk2, g_sub, lam_init, rope, ctx_out):
    H = HEADS_PER_GROUP
    cos, sin = rope
    scale = DIFF_D ** -0.5
    f32 = jnp.float32
    lam = (jnp.exp(jnp.sum(lq1.astype(f32) * lk1.astype(f32)))
           - jnp.exp(jnp.sum(lq2.astype(f32) * lk2.astype(f32))) + lam_init)

    def pair_heads(a):
        b, t, _ = a.shape
        return a.reshape(b, t, H, 2, DIFF_D).transpose(0, 2, 3, 1, 4)

    def attend(q, k, v):
        s = jnp.einsum('bhiqd,bhitd->bhiqt', q, k).astype(f32) * scale
        p = jax.nn.softmax(s, axis=-1)
        w = (p[:, :, 0] - lam * p[:, :, 1]).astype(v.dtype)
        return jnp.einsum('bhqt,bhtd->bhqd', w, v)

    def finish(o):
        return rms_norm(o, g_sub) * (1.0 - lam_init)

    ql = apply_rope(pair_heads(q_l), cos, sin)
    kl = apply_rope(pair_heads(k_l), cos, sin)
    kc = pair_heads(k_c)
    vc = to_heads(v_c, H)
    k_all = jnp.concatenate([kc, kl], axis=3)
    v_all = jnp.concatenate([vc, to_heads(v_l, H)], axis=2)
    ol = finish(over_query_blocks(lambda qb: attend(qb, k_all, v_all), ql))
    oc = None
    if ctx_out:
        oc = finish(attend(pair_heads(q_c), kc, vc))
    return ol, oc


def hybrid_mixer(h, hc, w_in, w_out, rpb, gqa_gq, gqa_gk, mla_gq, mla_gkv, mla_wuq, mla_wukv,
                 lq1, lk1, lq2, lk2, g_sub, lam_init, rope128, rope64, ctx_out):
    offsets = np.cumsum(IN_SPLITS)[:-1].tolist()
    pl = jnp.split(h @ w_in, offsets, axis=-1)
    pc = jnp.split(hc @ w_in, offsets, axis=-1)
    oa, oac = na_mixer(*pl[0:3], *pc[0:3], rpb, ctx_out)
    ob, obc = gqa_mixer(*pl[3:6], *pc[3:6], gqa_gq, gqa_gk, rope128, ctx_out)
    oc, occ = mla_mixer(*pl[6:9], *pc[6:9], mla_gq, mla_gkv, mla_wuq, mla_wukv, rope64, ctx_out)
    od, odc = diff_mixer(*pl[9:12], *pc[9:12], lq1, lk1, lq2, lk2, g_sub, lam_init, rope64, ctx_out)

    def merge(parts):
        o = jnp.concatenate(parts, axis=1)
        b, n, t, d = o.shape
        return o.transpose(0, 2, 1, 3).reshape(b, t, n * d) @ w_out

    out_c = merge([oac, obc, occ, odc]) if ctx_out else None
    return merge([oa, ob, oc, od]), out_c


def swiglu(h, wg, wu, wd):
    return (jax.nn.silu(h @ wg) * (h @ wu)) @ wd


def setup_inputs(seed: int = 0) -> dict:
    key = jax.random.key(seed)
    ks = iter(jax.random.split(key, 32))
    L, D, H = DEPTH, D_MODEL, HEADS_PER_GROUP

    def nrm(shape, scale):
        return jax.random.normal(next(ks), shape, jnp.float32) * scale

    def gain(shape):
        return 1.0 + 0.02 * jax.random.normal(next(ks), shape, jnp.float32)

    return {
        'x': nrm((BATCH, SEQ, D), 1.0),
        'c': nrm((BATCH, D), 1.0),
        'ctx': nrm((BATCH, CTX_LEN, D), 1.0),
        'c_ctx': nrm((D,), 1.0),
        'w_ada': nrm((L, D, 6 * D), D ** -0.5),
        'b_ada': nrm((L, 6 * D), 0.02),
        'g_attn': gain((L, D)),
        'g_ffn': gain((L, D)),
        'w_in': nrm((L, D, D_IN), D ** -0.5),
        'w_out': nrm((L, D_MIX, D), D_MIX ** -0.5),
        'na_rpb': nrm((L, H, 2 * NA_KH_MAX - 1, 2 * NA_KW - 1), 0.1),
        'gqa_gq': gain((L, HEAD_DIM)),
        'gqa_gk': gain((L, HEAD_DIM)),
        'mla_gq': gain((L, MLA_Q_RANK)),
        'mla_gkv': gain((L, MLA_KV_RANK)),
        'mla_wuq': nrm((L, MLA_Q_RANK, H * (MLA_NOPE + MLA_ROPE)), MLA_Q_RANK ** -0.5),
        'mla_wukv': nrm((L, MLA_KV_RANK, H * (MLA_NOPE + MLA_V)), MLA_KV_RANK ** -0.5),
        'diff_lq1': nrm((L, DIFF_D), 0.1),
        'diff_lk1': nrm((L, DIFF_D), 0.1),
        'diff_lq2': nrm((L, DIFF_D), 0.1),
        'diff_lk2': nrm((L, DIFF_D), 0.1),
        'diff_gsub': gain((L, 2 * DIFF_D)),
        'ffn_wg': nrm((L, D, D_FF), D ** -0.5),
        'ffn_wu': nrm((L, D, D_FF), D ** -0.5),
        'ffn_wd': nrm((L, D_FF, D), D_FF ** -0.5),
        'g_final': gain((D,)),
    }


def reference(x, c, ctx, c_ctx, w_ada, b_ada, g_attn, g_ffn, w_in, w_out, na_rpb, gqa_gq, gqa_gk,
              mla_gq, mla_gkv, mla_wuq, mla_wukv, diff_lq1, diff_lk1, diff_lq2, diff_lk2, diff_gsub,
              ffn_wg, ffn_wu, ffn_wd, g_final):
    s = x.shape[1]
    rope128 = axial_rope(s, HEAD_DIM)
    rope64 = axial_rope(s, MLA_ROPE)
    silu_c = jax.nn.silu(c)
    silu_cc = jax.nn.silu(c_ctx)
    xc = ctx
    for l in range(DEPTH):
        ctx_out = l < DEPTH - 1
        lam_init = 0.8 - 0.6 * math.exp(-0.3 * l)
        mod = (silu_c @ w_ada[l] + b_ada[l])[:, None, :]
        mod_c = silu_cc @ w_ada[l] + b_ada[l]
        sh_a, sc_a, gt_a, sh_f, sc_f, gt_f = jnp.split(mod, 6, axis=-1)
        csh_a, csc_a, cgt_a, csh_f, csc_f, cgt_f = jnp.split(mod_c, 6, axis=-1)
        h = rms_norm(x, g_attn[l]) * (1.0 + sc_a) + sh_a
        hc = rms_norm(xc, g_attn[l]) * (1.0 + csc_a) + csh_a
        o_l, o_c = hybrid_mixer(h, hc, w_in[l], w_out[l], na_rpb[l], gqa_gq[l], gqa_gk[l],
                                mla_gq[l], mla_gkv[l], mla_wuq[l], mla_wukv[l],
                                diff_lq1[l], diff_lk1[l], diff_lq2[l], diff_lk2[l], diff_gsub[l],
                                lam_init, rope128, rope64, ctx_out)
        x = x + gt_a * o_l
        h = rms_norm(x, g_ffn[l]) * (1.0 + sc_f) + sh_f
        x = x + gt_f * swiglu(h, ffn_wg[l], ffn_wu[l], ffn_wd[l])
        if ctx_out:
            xc = xc + cgt_a * o_c
            hc = rms_norm(xc, g_ffn[l]) * (1.0 + csc_f) + csh_f
            xc = xc + cgt_f * swiglu(hc, ffn_wg[l], ffn_wu[l], ffn_wd[l])
    return rms_norm(x, g_final)
```

```python
import math
import os
KSKIP = set(os.environ.get('KSKIP', '').split(','))
from contextlib import ExitStack

import numpy as np
import concourse.bass as bass
import concourse.mybir as mybir
from concourse.bass_utils import run_bass_kernel_spmd

F32 = mybir.dt.float32
BF16 = mybir.dt.bfloat16
AF = mybir.ActivationFunctionType
ALU = mybir.AluOpType
AX = mybir.AxisListType

D = 2048
DC = 16
CTX = 256
DIN = 4928
DFF = 5632
FC = 44
EPS = 1e-6
NCORES = 4
EPOCH = 12000
NEG = -30000.0

GROUPS = [(0, 512), (512, 512), (1024, 512), (1536, 512), (2048, 512), (2560, 512),
          (3072, 320), (3392, 512), (3904, 512), (4416, 512)]
NQC = 20
NKC = 15
VW = 1792


class Res:
    __slots__ = ("name", "w", "p", "r")

    def __init__(self, name=""):
        self.name = name
        self.w = {}
        self.p = {}
        self.r = {}


def _add(d, tok):
    key = id(tok[1])
    old = d.get(key)
    if old is None or tok[0] == "d" or tok[2] > old[2]:
        d[key] = tok


class DmaSem:
    def __init__(self, kb, name):
        self.kb = kb
        self.name = name
        self._sem = None
        self.issued = 0

    @property
    def sem(self):
        if self._sem is None:
            self._sem = self.kb.new_sem(self.name)
        return self._sem


class Eng:
    def __init__(self, kb, name, h):
        self.kb = kb
        self.name = name
        self.h = h
        self.seq = 0
        self.sems = []
        self.waited = {}

    def next_token(self):
        idx = self.seq // EPOCH
        while len(self.sems) <= idx:
            self.sems.append(self.kb.new_sem(f"{self.name}_e{len(self.sems)}"))
        self.seq += 1
        return ("e", self.sems[idx], (self.seq - 1) % EPOCH + 1, self.name)


class KB:
    def __init__(self, nc):
        self.nc = nc
        self.es = ExitStack()
        self.nsem = 0
        self._keep = []
        self.eng = {
            "pe": Eng(self, "pe", nc.tensor),
            "act": Eng(self, "act", nc.scalar),
            "dve": Eng(self, "dve", nc.vector),
            "pool": Eng(self, "pool", nc.gpsimd),
            "sp": Eng(self, "sp", nc.sync),
        }
        self.dsems = []
        self._dsem_by_name = {}

    def new_sem(self, name):
        s = self.es.enter_context(self.nc.semaphore(f"s{self.nsem}_{name}"))
        self.nsem += 1
        self._keep.append(s)
        return s

    def dsem(self, name):
        if name in self._dsem_by_name:
            return self._dsem_by_name[name]
        d = DmaSem(self, name)
        self.dsems.append(d)
        self._dsem_by_name[name] = d
        return d

    def wait(self, eng, tok):
        kind, so, val, _ = tok
        if kind == "d":
            sem = so.sem
            val = so.issued * 16
        else:
            sem = so
        key = id(so)
        if eng.waited.get(key, 0) >= val:
            return
        eng.h.wait_ge(sem, val)
        eng.waited[key] = val

    def _deps(self, en, reads, writes, pwrites):
        eng = self.eng[en]
        dw, dr = [], []
        for r in reads:
            dw += r.w.values()
            dw += r.p.values()
        for w in writes:
            dw += w.w.values()
            dw += w.p.values()
            dr += w.r.values()
        for w in pwrites:
            dr += w.r.values()
            dw += w.w.values()
            dw += [t for t in w.p.values() if t[0] == "d"]
        for t in dw:
            if t[0] == "e" and t[3] == en and en == "pe":
                continue
            self.wait(eng, t)
        for t in dr:
            if t[0] == "e" and t[3] == en and en == "pe":
                continue
            self.wait(eng, t)

    def _commit(self, tok, reads, writes, pwrites):
        for r in reads:
            _add(r.r, tok)
        for w in writes:
            w.w = {id(tok[1]): tok}
            w.p = {}
            w.r = {}
        for w in pwrites:
            _add(w.p, tok)

    def op(self, en, fn, reads=(), writes=(), pwrites=()):
        self._deps(en, reads, writes, pwrites)
        ins = fn()
        tok = self.eng[en].next_token()
        ins.then_inc(tok[1], 1)
        self._commit(tok, reads, writes, pwrites)
        return tok

    def peg(self, fns, reads=(), writes=(), pwrites=()):
        self._deps("pe", reads, writes, pwrites)
        ins = None
        for f in fns:
            ins = f()
        tok = self.eng["pe"].next_token()
        ins.then_inc(tok[1], 1)
        self._commit(tok, reads, writes, pwrites)
        return tok

    def dma(self, q, out, in_, dsem, reads=(), writes=(), pwrites=()):
        self._deps(q, reads, writes, pwrites)
        dsem = self.dsem(dsem.name + "_" + q) if not dsem.name.endswith("_" + q) else dsem
        if dsem.issued > 0:
            self.wait(self.eng[q], ("d", dsem, None, None))
        ins = self.eng[q].h.dma_start(out=out, in_=in_)
        dsem.issued += 1
        ins.then_inc(dsem.sem, 16)
        tok = ("d", dsem, None, None)
        self._commit(tok, reads, writes, pwrites)
        return tok

    def barrier(self):
        toks = []
        for e in self.eng.values():
            if e.seq > 0:
                idx = (e.seq - 1) // EPOCH
                toks.append(("e", e.sems[idx], (e.seq - 1) % EPOCH + 1, e.name))
        for d in self.dsems:
            if d.issued:
                toks.append(("d", d, None, None))
        for e in self.eng.values():
            for t in toks:
                if t[0] == "e" and t[3] == e.name:
                    continue
                self.wait(e, t)


class Ring:
    def __init__(self, items):
        self.items = items
        self.i = 0

    def next(self):
        it = self.items[self.i % len(self.items)]
        self.i += 1
        return it


def build_program(S, n_layers=2, debug=False, stop=9):
    T = CTX + S
    NT = T // 128
    NBLK = S // 512
    ROWS = S // 64
    blocks = [(0, 2)] + [(2 + 4 * i, 4) for i in range(NBLK)]

    nc = bass.Bass("TRN2", target_bir_lowering=False)
    kb = KB(nc)

    declared = []
    nc._declared_inputs = declared

    def din(name, shape, dt=F32):
        declared.append(name)
        return nc.dram_tensor(name, list(shape), dt, kind="ExternalInput").ap()

    class LazyIn:
        def __init__(self, name, shape):
            self.name, self.shape, self._ap = name, shape, None

        def __getitem__(self, key):
            if self._ap is None:
                self._ap = din(self.name, self.shape)
            return self._ap[key]

    skind = "ExternalOutput" if debug else "Internal"

    def dscr(name, shape, dt):
        return nc.dram_tensor(name, list(shape), dt, kind=skind).ap()

    xin = din("xin", [T, D])
    cT_d = din("cT", [128, DC, 2])
    w_ada = LazyIn("w_ada", [2, D, 6 * D])
    b_adaT = din("b_adaT", [2, 128, 96])
    g_attnT = din("g_attnT", [2, 128, DC])
    g_ffnT = din("g_ffnT", [2, 128, DC])
    g_final = din("g_final", [1, D])
    w_in = LazyIn("w_in", [2, D, DIN])
    w_out = LazyIn("w_out", [2, D, D])
    ffn_wg = LazyIn("ffn_wg", [2, D, DFF])
    ffn_wu = LazyIn("ffn_wu", [2, D, DFF])
    ffn_wd = LazyIn("ffn_wd", [2, DFF, D])
    nab = LazyIn("nab", [2, 4, 3, 8, 128, 512])
    gqa_g = din("gqa_g", [2, 2, 128])
    mla_gqT = din("mla_gqT", [2, 128, 4])
    mla_gkvT = din("mla_gkvT", [2, 128, 2])
    mla_wuq = din("mla_wuq", [2, 512, 768])
    mla_wukv = din("mla_wukv", [2, 256, 1024])
    diff_l = din("diff_l", [2, 4, 64])
    diff_gsubT = din("diff_gsubT", [128, 2])
    rope = din("rope", [T, 384])
    ident_d = din("ident", [128, 128])
    out_d = nc.dram_tensor("out", [S, D], F32, kind="ExternalOutput").ap()

    xres = dscr("xres", [T, D], F32)
    QT = dscr("QT", [NQC, 128, T], BF16)
    KT = dscr("KT", [NKC, 128, T], BF16)
    VV = dscr("VV", [T, VW], BF16)
    OT = dscr("OT", [DC, 128, T], BF16)
    H2T = dscr("H2T", [DC, 128, T], BF16)
    GT = dscr("GT", [2, 2, 2, D], F32)
    r_xres, r_QT, r_KT, r_VV, r_OT, r_H2T, r_GT = (Res(n) for n in
                                                     ("xres", "QT", "KT", "VV", "OT", "H2T", "GT"))
    r_out = Res("out")

    uid = [0]

    def sb(es, name, shape, dt):
        uid[0] += 1
        t = es.enter_context(nc.sbuf_tensor(f"{name}_u{uid[0]}", list(shape), dt))
        return t, Res(name)

    def pst(es, name, shape, dt):
        uid[0] += 1
        t = es.enter_context(nc.psum_tensor(f"{name}_u{uid[0]}", list(shape), dt))
        return t, Res(name)

    V = nc.vector
    A = nc.scalar
    P = nc.tensor

    ges = kb.es
    ident_f, r_ident_f = sb(ges, "ident_f", [128, 128], F32)
    ident_b, r_ident_b = sb(ges, "ident_b", [128, 128], BF16)
    ones_b, r_ones_b = sb(ges, "ones_b", [128, 128], BF16)
    modT = [sb(ges, f"modT{l}", [128, 96, 2], F32) for l in range(2)]
    Ga = [sb(ges, f"Ga{l}", [128, DC, 2], F32) for l in range(2)]
    Gf = [sb(ges, f"Gf{l}", [128, DC, 2], F32) for l in range(2)]
    neglam, r_neglam = sb(ges, "neglam", [128, 2], F32)
    gsc, r_gsc = sb(ges, "gsc", [128, 2], F32)
    ds_c = kb.dsem("const")

    kb.dma("sp", ident_f[:], ident_d[:, :], ds_c, writes=[r_ident_f])
    kb.dma("pool", ident_b[:], ident_d[:, :], ds_c, writes=[r_ident_b])
    kb.op("dve", lambda: V.memset(ones_b[:], 1.0), writes=[r_ones_b])

    with ExitStack() as es:
        cT, r_cT = sb(es, "cT", [128, DC, 2], F32)
        scT, r_scT = sb(es, "scT", [128, DC, 2], F32)
        bT, r_bT = sb(es, "bT", [128, 96], F32)
        gaT, r_gaT = sb(es, "gaT", [128, DC], F32)
        gfT, r_gfT = sb(es, "gfT", [128, DC], F32)
        tmp1, r_tmp1 = sb(es, "tmp1", [128, DC, 2], F32)
        gts, r_gts = sb(es, "gts", [32, 128], F32)
        lqb, r_lqb = sb(es, "lqb", [128, 4, 64], F32)
        lpr, r_lpr = sb(es, "lpr", [128, 2, 64], F32)
        lsum, r_lsum = sb(es, "lsum", [128, 2], F32)
        lexp, r_lexp = sb(es, "lexp", [128, 2], F32)
        gsub, r_gsub = sb(es, "gsub", [128, 2], F32)
        wa_ring = Ring([sb(es, f"wa{i}", [128, DC, 256], F32) + (kb.dsem(f"wa{i}"),) for i in range(3)])
        psm, r_psm = pst(es, "psm", [128, 96, 2], F32)
        pst_, r_pst = pst(es, "pstr", [32, 128], F32)
        ds_m = kb.dsem("modmisc")
        ds_gt = kb.dsem("gtstore")

        kb.dma("sp", cT[:], cT_d[:, :, :], ds_m, writes=[r_cT])
        kb.op("act", lambda: A.activation(out=scT[:], in_=cT[:], func=AF.Silu), reads=[r_cT], writes=[r_scT])
        kb.dma("sp", gsub[:], diff_gsubT[:, :], ds_m, writes=[r_gsub])
        for l in range(n_layers):
            if "noM" in KSKIP:
                continue
            lam_init = 0.8 - 0.6 * math.exp(-0.3 * l)
            mT, r_mT = modT[l]
            kb.dma("sp", bT[:], b_adaT[l], ds_m, writes=[r_bT])
            kb.dma("sp", gaT[:], g_attnT[l], ds_m, writes=[r_gaT])
            kb.dma("sp", gfT[:], g_ffnT[l], ds_m, writes=[r_gfT])
            wav = w_ada[l].rearrange("(k p) n -> p k n", p=128)
            for piece in range(48):
                wa, r_wa, ds_wa = wa_ring.next()
                kb.dma("sp", wa[:], wav[:, :, piece * 256:(piece + 1) * 256], ds_wa, writes=[r_wa])
                fns = []
                for cc in range(2):
                    col = piece * 2 + cc
                    for k in range(DC):
                        fns.append(lambda cc=cc, col=col, k=k, wa=wa: P.matmul(
                            psm[:, col, :], wa[:, k, cc * 128:(cc + 1) * 128], scT[:, k, :],
                            start=(k == 0), stop=(k == DC - 1)))
                kb.peg(fns, reads=[r_wa, r_scT], pwrites=[r_psm])
            kb.op("dve", lambda: V.tensor_tensor(out=mT[:], in0=psm[:],
                                                 in1=bT[:].unsqueeze(2).broadcast_to([128, 96, 2]), op=ALU.add),
                  reads=[r_psm, r_bT], writes=[r_mT])
            for (Gt, r_G), gT, r_gT, c0 in ((Ga[l], gaT, r_gaT, 16), (Gf[l], gfT, r_gfT, 64)):
                kb.op("dve", lambda c0=c0: V.tensor_scalar(out=tmp1[:], in0=mT[:, c0:c0 + 16, :], scalar1=1.0,
                                                          scalar2=None, op0=ALU.add),
                      reads=[r_mT], writes=[r_tmp1])
                kb.op("dve", lambda Gt=Gt, gT=gT: V.tensor_tensor(
                    out=Gt[:], in0=tmp1[:], in1=gT[:].unsqueeze(2).broadcast_to([128, DC, 2]), op=ALU.mult),
                    reads=[r_tmp1, r_gT], writes=[r_G])
            for gi, c0 in enumerate((32, 80)):
                kb.peg([lambda c0=c0: P.transpose(pst_[:], mT[:, c0:c0 + 16, :].rearrange("p c j -> p (c j)"),
                                                  ident_f[:])],
                       reads=[r_mT, r_ident_f], writes=[r_pst])
                kb.op("dve", lambda: V.tensor_copy(out=gts[:], in_=pst_[:]), reads=[r_pst], writes=[r_gts])
                for j in range(2):
                    g_ = gts[:]
                    src = bass.AP(g_.tensor, g_.offset + j * g_.ap[0][0], [[2 * g_.ap[0][0], 16], [1, 128]])
                    kb.dma("sp", GT[l, gi, j].rearrange("(c p) -> c p", p=128), src, ds_gt,
                           reads=[r_gts], pwrites=[r_GT])
            kb.dma("sp", lqb[:].rearrange("p a d -> p (a d)"),
                   diff_l[l].rearrange("a d -> (a d)").unsqueeze(0).broadcast_to([128, 256]),
                   ds_m, writes=[r_lqb])
            kb.op("dve", lambda: V.tensor_tensor(out=lpr[:, 0, :], in0=lqb[:, 0, :], in1=lqb[:, 1, :], op=ALU.mult),
                  reads=[r_lqb], pwrites=[r_lpr])
            kb.op("dve", lambda: V.tensor_tensor(out=lpr[:, 1, :], in0=lqb[:, 2, :], in1=lqb[:, 3, :], op=ALU.mult),
                  reads=[r_lqb], pwrites=[r_lpr])
            kb.op("dve", lambda: V.tensor_reduce(out=lsum[:], in_=lpr[:], axis=AX.X, op=ALU.add),
                  reads=[r_lpr], writes=[r_lsum])
            kb.op("act", lambda: A.activation(out=lexp[:], in_=lsum[:], func=AF.Exp), reads=[r_lsum], writes=[r_lexp])
            kb.op("dve", lambda: V.scalar_tensor_tensor(out=neglam[:, l:l + 1], in0=lexp[:, 1:2], scalar=-lam_init,
                                                         in1=lexp[:, 0:1], op0=ALU.add, op1=ALU.subtract),
                  reads=[r_lexp], pwrites=[r_neglam])
            kb.op("dve", lambda: V.tensor_scalar(out=gsc[:, l:l + 1], in0=gsub[:, l:l + 1], scalar1=1.0 - lam_init,
                                                 scalar2=None, op0=ALU.mult),
                  reads=[r_gsub], pwrites=[r_gsc])
        kb.barrier()

    def rms_rstd(es_tiles, ss_ap, n, out_ap, res_ss, res_tmp, res_out, tmp_ap, extra_scale=1.0):
        kb.op("act", lambda: A.activation(out=tmp_ap, in_=ss_ap, func=AF.Sqrt, bias=EPS, scale=1.0 / n),
              reads=[res_ss], writes=[res_tmp])
        kb.op("dve", lambda: V.reciprocal(out=out_ap, in_=tmp_ap), reads=[res_tmp], writes=[res_out])
        if extra_scale != 1.0:
            kb.op("dve", lambda: V.tensor_scalar(out=out_ap, in0=out_ap, scalar1=float(extra_scale), scalar2=None,
                                                 op0=ALU.mult), reads=[res_out], writes=[res_out])

    def sumsq(src_ap, r_src, ngrp, junk_ap, r_junk_, out_ap, r_out_):
        kb.op("dve", lambda: V.tensor_tensor(out=junk_ap, in0=src_ap, in1=src_ap, op=ALU.mult),
              reads=[r_src], writes=[r_junk_])
        jv = junk_ap if ngrp == 1 else junk_ap.rearrange("p (g d) -> p g d", g=ngrp)
        kb.op("dve", lambda: V.tensor_reduce(out=out_ap, in_=jv, axis=AX.X, op=ALU.add),
              reads=[r_junk_], writes=[r_out_])

    for l in range(n_layers):
        xsrc = xin if l == 0 else xres
        r_xsrc = Res("xin") if l == 0 else r_xres
        mT, r_mT = modT[l]
        GaT, r_Ga = Ga[l]
        GfT, r_Gf = Gf[l]

        if stop < 1:
            continue
        with ExitStack() as es:
            xt_ring = Ring([sb(es, f"xt{i}", [128, D], F32) + (kb.dsem(f"xt{i}"),) for i in range(2)])
            rp_tiles = [sb(es, f"rp{i}", [128, 384], F32) + (kb.dsem(f"rp{i}"),) for i in range(4)]
            rps = {}
            junk, r_junk = sb(es, "junk", [128, D], BF16)
            xn_ring = Ring([sb(es, f"xn{i}", [128, D], BF16) for i in range(2)])
            st1, r_st1 = sb(es, "st1", [128, 8], F32)
            st2, r_st2 = sb(es, "st2", [128, 8], F32)
            st3, r_st3 = sb(es, "st3", [128, 8], F32)
            hT, r_hT = sb(es, "hT", [128, DC, 512], BF16)
            w_ring = Ring([sb(es, f"wi{i}", [128, DC, 512], BF16) + (kb.dsem(f"wi{i}"),) for i in range(3)])
            qst, r_qst = sb(es, "qst", [128, NQC, 512], BF16)
            kst, r_kst = sb(es, "kst", [128, NKC, 512], BF16)
            vst, r_vst = sb(es, "vst", [128, 4, VW], BF16)
            tm_ring = Ring([sb(es, f"tm{i}", [128, 512], BF16) for i in range(3)])
            f1, r_f1 = sb(es, "f1", [128, 512], F32)
            f2, r_f2 = sb(es, "f2", [128, 512], F32)
            f3, r_f3 = sb(es, "f3", [128, 512], F32)
            gqb, r_gqb = sb(es, "gqb", [128, 2, 128], F32)
            wuq, r_wuq = sb(es, "wuq", [128, 4, 768], BF16)
            q32, r_q32 = sb(es, "q32", [128, 768], F32)
            wukv, r_wukv = sb(es, "wukv", [128, 2, 1024], BF16)
            mgq, r_mgq = sb(es, "mgq", [128, 4], F32)
            mgkv, r_mgkv = sb(es, "mgkv", [128, 2], F32)
            cT2, r_cT2 = sb(es, "cT2", [128, 4, 128], BF16)
            pp_ring = Ring([pst(es, f"pp{i}", [128, 512], F32) for i in range(2)])
            pu0, r_pu0 = pst(es, "pu0", [128, 512], F32)
            pu1, r_pu1 = pst(es, "pu1", [128, 512], F32)
            pt_ring = Ring([pst(es, f"pt{i}", [128, 1024], BF16) for i in range(2)])
            ph_ring = Ring([pst(es, f"ph{i}", [128, 1024], BF16) for i in range(2)])
            ds_a = kb.dsem("a_misc")
            ds_st = kb.dsem("a_store")

            kb.op("dve", lambda: V.memset(qst[:], 0.0), writes=[r_qst])
            kb.op("dve", lambda: V.memset(kst[:], 0.0), writes=[r_kst])
            if "noprep" not in KSKIP:
              if True:
                  kb.dma("sp", gqb[:].rearrange("p a d -> p (a d)"),
                         gqa_g[l].rearrange("a d -> (a d)").unsqueeze(0).broadcast_to([128, 256]),
                         ds_a, writes=[r_gqb])
                  kb.dma("sp", mgq[:], mla_gqT[l], ds_a, writes=[r_mgq])
                  kb.dma("sp", mgkv[:], mla_gkvT[l], ds_a, writes=[r_mgkv])
                  for c in range(4):
                      xt, r_xt, ds_xt = xt_ring.next()
                      kb.dma("sp", xt[:, 0:768], mla_wuq[l, c * 128:(c + 1) * 128, :], ds_xt, writes=[r_xt])
                      kb.op("dve", lambda c=c: V.tensor_scalar(out=wuq[:, c, :], in0=xt[:, 0:768], scalar1=mgq[:, c:c + 1],
                                                               scalar2=None, op0=ALU.mult),
                            reads=[r_xt, r_mgq], pwrites=[r_wuq])
                  for c in range(2):
                      xt, r_xt, ds_xt = xt_ring.next()
                      kb.dma("sp", xt[:, 0:1024], mla_wukv[l, c * 128:(c + 1) * 128, :], ds_xt, writes=[r_xt])
                      kb.op("dve", lambda c=c: V.tensor_scalar(out=wukv[:, c, :], in0=xt[:, 0:1024],
                                                               scalar1=mgkv[:, c:c + 1], scalar2=None, op0=ALU.mult),
                            reads=[r_xt, r_mgkv], pwrites=[r_wukv])

            wiv = w_in[l].rearrange("(k p) n -> p k n", p=128)

            def transposes(view_fn, n, width, stage, r_stage, chunk0, ti, r_src, prows=128):
                pt, r_pt = pt_ring.next()
                fns = [lambda i=i: P.transpose(pt[0:width, i * 128:(i + 1) * 128], view_fn(i), ident_b[:])
                       for i in range(n)]
                kb.peg(fns, reads=[r_src, r_ident_b], writes=[r_pt])
                kb.op("dve", lambda: V.tensor_copy(
                    out=stage[0:width, chunk0:chunk0 + n, ti * 128:(ti + 1) * 128],
                    in_=pt[0:width, 0:n * 128].rearrange("p (c t) -> p c t", t=128)),
                    reads=[r_pt], pwrites=[r_stage])

            def rope_op(src_ap, r_src, U, hh, cs, sn, r_tab, out_ap, r_out_, r_out_partial=True):
                w = 2 * hh
                s4 = src_ap
                t1 = f2[:, 0:U * w].rearrange("p (u a d) -> p u a d", u=U, a=2)
                t2 = f3[:, 0:U * w].rearrange("p (u a d) -> p u a d", u=U, a=2)
                csb = cs.rearrange("p (a d) -> p a d", a=2).unsqueeze(1).broadcast_to([128, U, 2, hh])
                kb.op("dve", lambda: V.tensor_tensor(out=t1, in0=s4, in1=csb, op=ALU.mult),
                      reads=[r_src, r_tab], writes=[r_f2])
                for a in range(2):
                    snb = sn[:, a * hh:(a + 1) * hh].unsqueeze(1).broadcast_to([128, U, hh])
                    kb.op("dve", lambda a=a, snb=snb: V.tensor_tensor(out=t2[:, :, a, :], in0=s4[:, :, 1 - a, :],
                                                                      in1=snb, op=ALU.mult),
                          reads=[r_src, r_tab], pwrites=[r_f3])
                kw = dict(pwrites=[r_out_]) if r_out_partial else dict(writes=[r_out_])
                kb.op("dve", lambda: V.tensor_tensor(out=out_ap, in0=t1, in1=t2, op=ALU.add),
                      reads=[r_f2, r_f3], **kw)

            for bi, (t0, nt) in enumerate(blocks):
                jm = 1 if bi == 0 else 0
                nb = nt * 128
                for ti in range(nt):
                    tg = t0 + ti
                    xt, r_xt, ds_xt = xt_ring.next()
                    kb.dma("sp", xt[:], xsrc[tg * 128:(tg + 1) * 128, :], ds_xt, reads=[r_xsrc], writes=[r_xt])
                    rp, r_rp, ds_rp = rp_tiles[ti]
                    kb.dma("sp", rp[:], rope[tg * 128:(tg + 1) * 128, :], ds_rp, writes=[r_rp])
                    rps[ti] = (rp, r_rp)
                    sumsq(xt[:], r_xt, 1, junk[:], r_junk, st1[:, 0:1], r_st1)
                    rms_rstd(None, st1[:, 0:1], D, st3[:, 0:1], r_st1, r_st2, r_st3, st2[:, 0:1])
                    xn, r_xn = xn_ring.next()
                    kb.op("dve", lambda: V.tensor_scalar(out=xn[:], in0=xt[:], scalar1=st3[:, 0:1], scalar2=None,
                                                         op0=ALU.mult),
                          reads=[r_xt, r_st3], writes=[r_xn])
                    for half in range(2):
                        if "notr" in KSKIP:
                            continue
                        ph, r_ph = ph_ring.next()
                        kb.peg([lambda c=c: P.transpose(ph[:, c * 128:(c + 1) * 128],
                                                        xn[:, (half * 8 + c) * 128:(half * 8 + c + 1) * 128],
                                                        ident_b[:]) for c in range(8)],
                               reads=[r_xn, r_ident_b], writes=[r_ph])
                        for c in range(8):
                            cc = half * 8 + c
                            if "noevac" in KSKIP:
                                continue
                            if False:
                                kb.op("act", lambda c=c, cc=cc: A.activation(
                                    out=hT[:, cc, ti * 128:(ti + 1) * 128], in_=ph[:, c * 128:(c + 1) * 128],
                                    func=AF.Identity, scale=GaT[:, cc, jm:jm + 1], bias=mT[:, cc, jm:jm + 1]),
                                    reads=[r_ph, r_Ga, r_mT], pwrites=[r_hT])
                            else:
                                kb.op("dve", lambda c=c, cc=cc: V.tensor_scalar(
                                    out=hT[:, cc, ti * 128:(ti + 1) * 128], in0=ph[:, c * 128:(c + 1) * 128],
                                    scalar1=GaT[:, cc, jm:jm + 1], scalar2=mT[:, cc, jm:jm + 1],
                                    op0=ALU.mult, op1=ALU.add),
                                    reads=[r_ph, r_Ga, r_mT], pwrites=[r_hT])
                for g, (c0, cw) in enumerate(GROUPS):
                    if "a2" in KSKIP:
                        continue
                    ws, r_ws, ds_ws = w_ring.next()
                    kb.dma("pool", ws[:, :, 0:cw], wiv[:, :, c0:c0 + cw], ds_ws, writes=[r_ws])
                    for ti in range(nt):
                        rp, r_rp = rps[ti]
                        cs128, sn128 = rp[:, 0:128], rp[:, 128:256]
                        cs64, sn64 = rp[:, 256:320], rp[:, 320:384]
                        ps, r_ps = pp_ring.next()
                        kb.peg([lambda k=k: P.matmul(ps[:, 0:cw], hT[:, k, ti * 128:(ti + 1) * 128], ws[:, k, 0:cw],
                                                     start=(k == 0), stop=(k == DC - 1)) for k in range(DC)],
                               reads=[r_hT, r_ws], writes=[r_ps])
                        tsl = slice(ti * 128, (ti + 1) * 128)
                        if "post" in KSKIP or (f"g{g}" in KSKIP):
                            kb.op("dve", lambda: V.tensor_copy(out=vst[:, ti, 0:cw], in_=ps[:, 0:cw]),
                                  reads=[r_ps], pwrites=[r_vst])
                        elif g == 0 or g == 1:
                            tm, r_tm = tm_ring.next()
                            sc = 128 ** -0.5 if g == 0 else 1.0
                            kb.op("dve", lambda: V.tensor_scalar(out=tm[:], in0=ps[:], scalar1=float(sc), scalar2=None,
                                                                 op0=ALU.mult),
                                  reads=[r_ps], writes=[r_tm])
                            stage, r_stage = (qst, r_qst) if g == 0 else (kst, r_kst)
                            transposes(lambda i: tm[:, i * 128:(i + 1) * 128], 4, 128, stage, r_stage, 0, ti, r_tm)
                        elif g in (2, 9):
                            vc = 0 if g == 2 else 1280
                            kb.op("dve", lambda: V.tensor_copy(out=vst[:, ti, vc:vc + 512], in_=ps[:]),
                                  reads=[r_ps], pwrites=[r_vst])
                        elif g in (3, 4):
                            nh = 4 if g == 3 else 2
                            gi = 0 if g == 3 else 1
                            w = nh * 128
                            kb.op("dve", lambda: V.tensor_copy(out=f1[:, 0:w], in_=ps[:, 0:w]), reads=[r_ps], writes=[r_f1])
                            sumsq(f1[:, 0:w], r_f1, nh, f2[:, 0:w], r_f2, st1[:, 0:nh], r_st1)
                            rms_rstd(None, st1[:, 0:nh], 128, st3[:, 0:nh], r_st1, r_st2, r_st3, st2[:, 0:nh],
                                     extra_scale=(128 ** -0.5 if g == 3 else 1.0))
                            kb.op("dve", lambda: V.tensor_tensor(
                                out=f1[:, 0:w].rearrange("p (h d) -> p h d", h=nh),
                                in0=f1[:, 0:w].rearrange("p (h d) -> p h d", h=nh),
                                in1=st3[:, 0:nh].unsqueeze(2).broadcast_to([128, nh, 128]), op=ALU.mult),
                                reads=[r_f1, r_st3], writes=[r_f1])
                            kb.op("dve", lambda: V.tensor_tensor(
                                out=f1[:, 0:w].rearrange("p (h d) -> p h d", h=nh),
                                in0=f1[:, 0:w].rearrange("p (h d) -> p h d", h=nh),
                                in1=gqb[:, gi, :].unsqueeze(1).broadcast_to([128, nh, 128]), op=ALU.mult),
                                reads=[r_f1, r_gqb], writes=[r_f1])
                            tm, r_tm = tm_ring.next()
                            rope_op(f1[:, 0:w].rearrange("p (u a d) -> p u a d", u=nh, a=2), r_f1, nh, 64,
                                    cs128, sn128, r_rp,
                                    tm[:, 0:w].rearrange("p (u a d) -> p u a d", u=nh, a=2), r_tm, r_out_partial=False)
                            stage, r_stage = (qst, r_qst) if g == 3 else (kst, r_kst)
                            transposes(lambda i: tm[:, i * 128:(i + 1) * 128], nh, 128, stage, r_stage, 4, ti, r_tm)
                            if g == 4:
                                kb.op("dve", lambda: V.tensor_copy(out=vst[:, ti, 512:768], in_=ps[:, 256:512]),
                                      reads=[r_ps], pwrites=[r_vst])
                        elif g == 5:
                            kb.op("dve", lambda: V.tensor_copy(out=f1[:], in_=ps[:]), reads=[r_ps], writes=[r_f1])
                            sumsq(f1[:], r_f1, 1, f2[:], r_f2, st1[:, 0:1], r_st1)
                            rms_rstd(None, st1[:, 0:1], 512, st3[:, 0:1], r_st1, r_st2, r_st3, st2[:, 0:1])
                            tm, r_tm = tm_ring.next()
                            kb.op("dve", lambda: V.tensor_scalar(out=tm[:], in0=ps[:], scalar1=st3[:, 0:1],
                                                                 scalar2=None, op0=ALU.mult),
                                  reads=[r_ps, r_st3], writes=[r_tm])
                            pt, r_pt = pt_ring.next()
                            kb.peg([lambda i=i: P.transpose(pt[:, i * 128:(i + 1) * 128], tm[:, i * 128:(i + 1) * 128],
                                                            ident_b[:]) for i in range(4)],
                                   reads=[r_tm, r_ident_b], writes=[r_pt])
                            kb.op("dve", lambda: V.tensor_copy(out=cT2[:].rearrange("p c t -> p (c t)"), in_=pt[:, 0:512]),
                                  reads=[r_pt], writes=[r_cT2])
                            kb.peg([lambda c=c: P.matmul(pu0[:], cT2[:, c, :], wuq[:, c, 0:512], start=(c == 0),
                                                         stop=(c == 3)) for c in range(4)] +
                                   [lambda c=c: P.matmul(pu1[:, 0:256], cT2[:, c, :], wuq[:, c, 512:768],
                                                         start=(c == 0), stop=(c == 3)) for c in range(4)],
                                   reads=[r_cT2, r_wuq], writes=[r_pu0, r_pu1])
                            msc = 192 ** -0.5
                            kb.op("dve", lambda: V.tensor_scalar(out=q32[:, 0:512], in0=pu0[:], scalar1=float(msc),
                                                                 scalar2=None, op0=ALU.mult),
                                  reads=[r_pu0], pwrites=[r_q32])
                            kb.op("dve", lambda: V.tensor_scalar(out=q32[:, 512:768], in0=pu1[:, 0:256], scalar1=float(msc),
                                                                 scalar2=None, op0=ALU.mult),
                                  reads=[r_pu1], pwrites=[r_q32])
                            q3 = q32[:].rearrange("p (h x) -> p h x", h=4)
                            tm2, r_tm2 = tm_ring.next()
                            kb.op("dve", lambda: V.tensor_copy(out=tm2[:].rearrange("p (h d) -> p h d", h=4),
                                                               in_=q3[:, :, 0:128]),
                                  reads=[r_q32], writes=[r_tm2])
                            transposes(lambda i: tm2[:, i * 128:(i + 1) * 128], 4, 128, qst, r_qst, 8, ti, r_tm2)
                            tm3, r_tm3 = tm_ring.next()
                            rope_op(q3[:, :, 128:192].rearrange("p h (a d) -> p h a d", a=2), r_q32, 4, 32,
                                    cs64, sn64, r_rp,
                                    tm3[:, 0:256].rearrange("p (u a d) -> p u a d", u=4, a=2), r_tm3,
                                    r_out_partial=False)
                            transposes(lambda i: tm3[:, i * 64:(i + 1) * 64], 4, 64, qst, r_qst, 12, ti, r_tm3)
                        elif g == 6:
                            kb.op("dve", lambda: V.tensor_copy(out=f1[:, 0:256], in_=ps[:, 0:256]), reads=[r_ps], writes=[r_f1])
                            sumsq(f1[:, 0:256], r_f1, 1, f2[:, 0:256], r_f2, st1[:, 0:1], r_st1)
                            rms_rstd(None, st1[:, 0:1], 256, st3[:, 0:1], r_st1, r_st2, r_st3, st2[:, 0:1])
                            tm, r_tm = tm_ring.next()
                            kb.op("dve", lambda: V.tensor_scalar(out=tm[:, 0:256], in0=ps[:, 0:256], scalar1=st3[:, 0:1],
                                                                 scalar2=None, op0=ALU.mult),
                                  reads=[r_ps, r_st3], writes=[r_tm])
                            pt, r_pt = pt_ring.next()
                            kb.peg([lambda i=i: P.transpose(pt[:, i * 128:(i + 1) * 128], tm[:, i * 128:(i + 1) * 128],
                                                            ident_b[:]) for i in range(2)],
                                   reads=[r_tm, r_ident_b], writes=[r_pt])
                            kb.op("dve", lambda: V.tensor_copy(out=cT2[:, 0:2, :].rearrange("p c t -> p (c t)"),
                                                               in_=pt[:, 0:256]),
                                  reads=[r_pt], writes=[r_cT2])
                            kb.peg([lambda c=c: P.matmul(pu0[:], cT2[:, c, :], wukv[:, c, 0:512], start=(c == 0),
                                                         stop=(c == 1)) for c in range(2)] +
                                   [lambda c=c: P.matmul(pu1[:], cT2[:, c, :], wukv[:, c, 512:1024], start=(c == 0),
                                                         stop=(c == 1)) for c in range(2)],
                                   reads=[r_cT2, r_wukv], writes=[r_pu0, r_pu1])
                            tm2, r_tm2 = tm_ring.next()
                            for hp, (pu, r_pu) in enumerate(((pu0, r_pu0), (pu1, r_pu1))):
                                pv = pu[:].rearrange("p (h x) -> p h x", h=2)
                                kb.op("dve", lambda pv=pv, hp=hp: V.tensor_copy(
                                    out=tm2[:, hp * 256:(hp + 1) * 256].rearrange("p (h d) -> p h d", h=2),
                                    in_=pv[:, :, 0:128]), reads=[r_pu], pwrites=[r_tm2])
                                kb.op("dve", lambda pv=pv, hp=hp: V.tensor_copy(
                                    out=vst[:, ti, 768 + hp * 256:768 + (hp + 1) * 256].rearrange("p (h d) -> p h d", h=2),
                                    in_=pv[:, :, 128:256]), reads=[r_pu], pwrites=[r_vst])
                            transposes(lambda i: tm2[:, i * 128:(i + 1) * 128], 4, 128, kst, r_kst, 6, ti, r_tm2)
                            tm3, r_tm3 = tm_ring.next()
                            rope_op(ps[:, 256:320].rearrange("p (u a d) -> p u a d", u=1, a=2), r_ps, 1, 32,
                                    cs64, sn64, r_rp,
                                    tm3[:, 0:64].rearrange("p (u a d) -> p u a d", u=1, a=2), r_tm3,
                                    r_out_partial=False)
                            transposes(lambda i: tm3[:, 0:64], 1, 64, kst, r_kst, 10, ti, r_tm3)
                        elif g in (7, 8):
                            if g == 7:
                                kb.op("dve", lambda: V.tensor_scalar(out=f1[:], in0=ps[:], scalar1=float(64 ** -0.5),
                                                                     scalar2=None, op0=ALU.mult),
                                      reads=[r_ps], writes=[r_f1])
                                src, r_src = f1[:], r_f1
                            else:
                                src, r_src = ps[:], r_ps
                            tm, r_tm = tm_ring.next()
                            rope_op(src.rearrange("p (u a d) -> p u a d", u=8, a=2), r_src, 8, 32, cs64, sn64, r_rp,
                                    tm[:].rearrange("p (u a d) -> p u a d", u=8, a=2), r_tm, r_out_partial=False)
                            stage, r_stage = (qst, r_qst) if g == 7 else (kst, r_kst)
                            transposes(lambda i: tm[:, i * 128:(i + 1) * 128], 4, 128, stage, r_stage,
                                       16 if g == 7 else 11, ti, r_tm)
                tok0 = t0 * 128
                if "nostore" in KSKIP:
                    continue
                kb.dma("sp", QT[:, :, tok0:tok0 + nb].rearrange("c p t -> p c t"), qst[:, :, 0:nb], kb.dsem("a_stq"),
                       reads=[r_qst], pwrites=[r_QT])
                kb.dma("sp", KT[:, :, tok0:tok0 + nb].rearrange("c p t -> p c t"), kst[:, :, 0:nb], kb.dsem("a_stk"),
                       reads=[r_kst], pwrites=[r_KT])
                kb.dma("sp", VV[tok0:tok0 + nb, :].rearrange("(t p) c -> p t c", p=128), vst[:, 0:nt, :], kb.dsem("a_stv"),
                       reads=[r_vst], pwrites=[r_VV])
            kb.barrier()

        if stop < 2:
            continue
        with ExitStack() as es:
            k_ring = Ring([sb(es, f"kb{i}", [128, 2, T], BF16) + (kb.dsem(f"kb{i}"),) for i in range(2)])
            v_ring = Ring([sb(es, f"vb{i}", [128, NT, 128], BF16) + (kb.dsem(f"vb{i}"),) for i in range(2)])
            q_ring = Ring([sb(es, f"qb{i}", [128, 2, 512], BF16) + (kb.dsem(f"qb{i}"),) for i in range(2)])
            b_ring = Ring([sb(es, f"nb{i}", [128, 8, 512], BF16) + (kb.dsem(f"nb{i}"),) for i in range(2)])
            p_ring = Ring([sb(es, f"pb{i}", [128, 512], BF16) for i in range(3)])
            o_ring = Ring([sb(es, f"ob{i}", [128, 512], BF16) + (kb.dsem(f"ob{i}"),) for i in range(2)])
            rs0, r_rs0 = sb(es, "rs0", [128, 512], F32)
            rs1, r_rs1 = sb(es, "rs1", [128, 512], F32)
            of0, r_of0 = sb(es, "of0", [128, 512], F32)
            of1, r_of1 = sb(es, "of1", [128, 512], F32)
            sqb, r_sqb = sb(es, "sqb", [128, 512], BF16)
            s_ring = Ring([pst(es, f"sp{i}", [128, 512], F32) for i in range(3)])
            accO = [pst(es, f"ao{i}", [128, 512], F32) for i in range(2)]
            accS = [pst(es, f"as{i}", [128, 512], F32) for i in range(2)]
            pss, r_pss = pst(es, "pss", [128, 512], F32)

            heads = []
            for h in range(4):
                heads.append(dict(kind="na", h=h, kch=[h], maps=[[(h, 0, 0, 128)]], vcol=h * 128, och=h))
            for h in range(4):
                heads.append(dict(kind="gqa", h=h, kch=[4 + h // 2], maps=[[(4 + h, 0, 0, 128)]],
                                  vcol=512 + (h // 2) * 128, och=4 + h))
            for h in range(4):
                heads.append(dict(kind="mla", h=h, kch=[6 + h, 10], maps=[[(8 + h, 0, 0, 128), (12 + h, 1, 0, 64)]],
                                  vcol=768 + h * 128, och=8 + h))
            for h in range(4):
                heads.append(dict(kind="diff", h=h, kch=[11 + h],
                                  maps=[[(16 + h, 0, 0, 64)], [(16 + h, 0, 64, 64)]], vcol=1280 + h * 128, och=12 + h))

            for hd in heads:
                kt_, r_kt, ds_kt = k_ring.next()
                for i, ch in enumerate(hd["kch"]):
                    kb.dma("sp", kt_[:, i, :], KT[ch], ds_kt, reads=[r_KT],
                           **(dict(writes=[r_kt]) if i == 0 else dict(pwrites=[r_kt])))
                vt_, r_vt, ds_vt = v_ring.next()
                kb.dma("sp", vt_[:], VV[:, hd["vcol"]:hd["vcol"] + 128].rearrange("(t p) c -> p t c", p=128), ds_vt,
                       reads=[r_VV], writes=[r_vt])
                qchs = sorted({m[0] for mp in hd["maps"] for m in mp})
                nmaps = len(hd["maps"])
                for bi, (t0, nt) in enumerate(blocks):
                    if bi == 0 and l == n_layers - 1 and n_layers == 2:
                        continue
                    nq = nt * 128
                    tok0 = t0 * 128
                    qt_, r_qt, ds_qt = q_ring.next()
                    for i, ch in enumerate(qchs):
                        kb.dma("sp", qt_[:, i, 0:nq], QT[ch, :, tok0:tok0 + nq], ds_qt, reads=[r_QT],
                               **(dict(writes=[r_qt]) if i == 0 else dict(pwrites=[r_qt])))
                    qidx = {ch: i for i, ch in enumerate(qchs)}
                    if bi == 0:
                        ktiles = [(0, None), (1, None)]
                    elif hd["kind"] == "na":
                        gq = bi - 1
                        base = min(max(8 * gq - 4, 0), ROWS - 16)
                        pat = 0 if gq == 0 else (2 if gq == NBLK - 1 else 1)
                        bt_, r_bt, ds_bt = b_ring.next()
                        kb.dma("pool", bt_[:], nab[l, hd["h"], pat].rearrange("j k q -> k j q"), ds_bt, writes=[r_bt])
                        ktiles = [(0, None), (1, None)] + [(2 + base // 2 + j, j) for j in range(8)]
                    else:
                        ktiles = [(t, None) for t in range(NT)]
                    units = [(kt, bj, m) for (kt, bj) in ktiles for m in range(nmaps)]
                    nu = len(units)
                    sres = {}

                    def QK(u):
                        kt, bj, m = units[u]
                        sp_, r_sp = s_ring.next()
                        sres[u] = (sp_, r_sp)
                        fns = []
                        mp = hd["maps"][m]
                        tot = len(mp) + (1 if bj is not None else 0)
                        for i, (qch, kci, p0, pn) in enumerate(mp):
                            fns.append(lambda i=i, qch=qch, kci=kci, p0=p0, pn=pn: P.matmul(
                                sp_[:, 0:nq], kt_[p0:p0 + pn, kci, kt * 128:(kt + 1) * 128],
                                qt_[p0:p0 + pn, qidx[qch], 0:nq], start=(i == 0), stop=(i == tot - 1)))
                        rd = [r_kt, r_qt]
                        if bj is not None:
                            fns.append(lambda: P.matmul(sp_[:, 0:nq], ident_b[:], bt_[:, bj, 0:nq],
                                                        start=False, stop=True))
                            rd += [r_bt, r_ident_b]
                        kb.peg(fns, reads=rd, writes=[r_sp])

                    def EXP(u):
                        sp_, r_sp = sres[u]
                        pb, r_pb = p_ring.next()
                        sres[u] = (pb, r_pb)
                        kb.op("act", lambda: A.activation(out=pb[:, 0:nq], in_=sp_[:, 0:nq], func=AF.Exp),
                              reads=[r_sp], writes=[r_pb])

                    def PV(u):
                        kt, bj, m = units[u]
                        pb, r_pb = sres.pop(u)
                        first = (u < nmaps)
                        last = (u >= nu - nmaps)
                        ao, r_ao = accO[m]
                        as_, r_as = accS[m]
                        kb.peg([lambda: P.matmul(ao[:, 0:nq], vt_[:, kt, :], pb[:, 0:nq], start=first, stop=last),
                                lambda: P.matmul(as_[:, 0:nq], ones_b[:], pb[:, 0:nq], start=first, stop=last)],
                               reads=[r_pb, r_vt, r_ones_b], writes=[r_ao, r_as])

                    QK(0)
                    if nu > 1:
                        QK(1)
                    for u in range(nu):
                        EXP(u)
                        if u + 2 < nu:
                            QK(u + 2)
                        PV(u)
                    ob, r_ob, ds_ob = o_ring.next()
                    if nmaps == 1:
                        kb.op("dve", lambda: V.reciprocal(out=rs0[:, 0:nq], in_=accS[0][0][:, 0:nq]),
                              reads=[accS[0][1]], writes=[r_rs0])
                        kb.op("dve", lambda: V.tensor_tensor(out=ob[:, 0:nq], in0=accO[0][0][:, 0:nq], in1=rs0[:, 0:nq],
                                                             op=ALU.mult),
                              reads=[accO[0][1], r_rs0], writes=[r_ob])
                    else:
                        kb.op("dve", lambda: V.reciprocal(out=rs0[:, 0:nq], in_=accS[0][0][:, 0:nq]),
                              reads=[accS[0][1]], writes=[r_rs0])
                        kb.op("dve", lambda: V.reciprocal(out=rs1[:, 0:nq], in_=accS[1][0][:, 0:nq]),
                              reads=[accS[1][1]], writes=[r_rs1])
                        kb.op("dve", lambda: V.tensor_tensor(out=of0[:, 0:nq], in0=accO[0][0][:, 0:nq], in1=rs0[:, 0:nq],
                                                             op=ALU.mult),
                              reads=[accO[0][1], r_rs0], writes=[r_of0])
                        kb.op("dve", lambda: V.tensor_tensor(out=of1[:, 0:nq], in0=accO[1][0][:, 0:nq], in1=rs1[:, 0:nq],
                                                             op=ALU.mult),
                              reads=[accO[1][1], r_rs1], writes=[r_of1])
                        kb.op("dve", lambda: V.scalar_tensor_tensor(out=of0[:, 0:nq], in0=of1[:, 0:nq],
                                                                    scalar=neglam[:, l:l + 1], in1=of0[:, 0:nq],
                                                                    op0=ALU.mult, op1=ALU.add),
                              reads=[r_of1, r_of0, r_neglam], writes=[r_of0])
                        kb.op("dve", lambda: V.tensor_tensor(out=sqb[:, 0:nq], in0=of0[:, 0:nq], in1=of0[:, 0:nq], op=ALU.mult),
                              reads=[r_of0], writes=[r_sqb])
                        kb.peg([lambda: P.matmul(pss[:, 0:nq], ones_b[:], sqb[:, 0:nq], start=True, stop=True)],
                               reads=[r_sqb, r_ones_b], writes=[r_pss])
                        kb.op("dve", lambda: V.tensor_copy(out=rs1[:, 0:nq], in_=pss[:, 0:nq]), reads=[r_pss], writes=[r_rs1])
                        kb.op("act", lambda: A.activation(out=rs0[:, 0:nq], in_=rs1[:, 0:nq], func=AF.Sqrt, bias=EPS,
                                                          scale=1.0 / 128), reads=[r_rs1], writes=[r_rs0])
                        kb.op("dve", lambda: V.reciprocal(out=rs1[:, 0:nq], in_=rs0[:, 0:nq]),
                              reads=[r_rs0], writes=[r_rs1])
                        kb.op("dve", lambda: V.tensor_tensor(out=of1[:, 0:nq], in0=of0[:, 0:nq], in1=rs1[:, 0:nq],
                                                             op=ALU.mult), reads=[r_of0, r_rs1], writes=[r_of1])
                        kb.op("dve", lambda: V.tensor_scalar(out=ob[:, 0:nq], in0=of1[:, 0:nq], scalar1=gsc[:, l:l + 1],
                                                             scalar2=None, op0=ALU.mult),
                              reads=[r_of1, r_gsc], writes=[r_ob])
                    kb.dma("sp", OT[hd["och"], :, tok0:tok0 + nq], ob[:, 0:nq], ds_ob, reads=[r_ob], pwrites=[r_OT])
            kb.barrier()

        if stop < 3:
            continue
        cblocks = [b for bi, b in enumerate(blocks) if not (bi == 0 and l == n_layers - 1 and n_layers == 2)]
        with ExitStack() as es:
            wo, r_wo = sb(es, "wo", [128, DC, D], BF16)
            ds_wo = kb.dsem("wo")
            gl, r_gl = sb(es, "gl", [128, D], F32)
            gc, r_gc = sb(es, "gc", [128, D], F32)
            o_ring = Ring([sb(es, f"osb{i}", [128, DC, 512], BF16) + (kb.dsem(f"osb{i}"),) for i in range(2)])
            xt_ring = Ring([sb(es, f"cxt{i}", [128, D], F32) + (kb.dsem(f"cxt{i}"), kb.dsem(f"cxs{i}"))
                            for i in range(2)])
            xn_ring = Ring([sb(es, f"cxn{i}", [128, D], BF16) for i in range(2)])
            junk, r_junk = sb(es, "cjunk", [128, D], BF16)
            tmp, r_tmp = sb(es, "ctmp", [128, 512], F32)
            st1, r_st1 = sb(es, "cst1", [128, 8], F32)
            st2, r_st2 = sb(es, "cst2", [128, 8], F32)
            st3, r_st3 = sb(es, "cst3", [128, 8], F32)
            h2_ring = Ring([sb(es, f"h2s{i}", [128, DC, 512], BF16) + (kb.dsem(f"h2s{i}"),) for i in range(2)])
            pp_ring = Ring([pst(es, f"cpp{i}", [128, 512], F32) for i in range(3)])
            ph_ring = Ring([pst(es, f"cph{i}", [128, 1024], BF16) for i in range(2)])
            ds_g = kb.dsem("c_gates")

            wov = w_out[l].rearrange("(k p) n -> p k n", p=128)
            for q4 in range(4):
                kb.dma("pool", wo[:, :, q4 * 512:(q4 + 1) * 512], wov[:, :, q4 * 512:(q4 + 1) * 512], ds_wo,
                       **(dict(writes=[r_wo]) if q4 == 0 else dict(pwrites=[r_wo])))
            kb.dma("sp", gl[:], GT[l, 0, 0].unsqueeze(0).broadcast_to([128, D]), ds_g, reads=[r_GT], writes=[r_gl])
            kb.dma("sp", gc[:], GT[l, 0, 1].unsqueeze(0).broadcast_to([128, D]), ds_g, reads=[r_GT], writes=[r_gc])

            for (t0, nt) in cblocks:
                is_ctx = (t0 == 0)
                jm = 1 if is_ctx else 0
                gate = gc if is_ctx else gl
                r_gate = r_gc if is_ctx else r_gl
                nb = nt * 128
                tok0 = t0 * 128
                osb, r_osb, ds_osb = o_ring.next()
                kb.dma("sp", osb[:, :, 0:nb], OT[:, :, tok0:tok0 + nb].rearrange("c p t -> p c t"), ds_osb,
                       reads=[r_OT], writes=[r_osb])
                h2s, r_h2s, ds_h2s = h2_ring.next()
                for ti in range(nt):
                    tg = t0 + ti
                    xt, r_xt, ds_xt, ds_xs = xt_ring.next()
                    kb.dma("sp", xt[:], xsrc[tg * 128:(tg + 1) * 128, :], ds_xt, reads=[r_xsrc], writes=[r_xt])
                    for cg in range(4):
                        ps, r_ps = pp_ring.next()
                        kb.peg([lambda k=k: P.matmul(ps[:], osb[:, k, ti * 128:(ti + 1) * 128],
                                                     wo[:, k, cg * 512:(cg + 1) * 512], start=(k == 0),
                                                     stop=(k == DC - 1)) for k in range(DC)],
                               reads=[r_osb, r_wo], writes=[r_ps])
                        kb.op("dve", lambda: V.tensor_tensor(out=tmp[:], in0=ps[:], in1=gate[:, cg * 512:(cg + 1) * 512],
                                                             op=ALU.mult), reads=[r_ps, r_gate], writes=[r_tmp])
                        kb.op("dve", lambda: V.tensor_tensor(out=xt[:, cg * 512:(cg + 1) * 512],
                                                             in0=xt[:, cg * 512:(cg + 1) * 512], in1=tmp[:], op=ALU.add),
                              reads=[r_tmp, r_xt], writes=[r_xt])
                    kb.dma("sp", xres[tg * 128:(tg + 1) * 128, :], xt[:], ds_xs, reads=[r_xt], pwrites=[r_xres])
                    sumsq(xt[:], r_xt, 1, junk[:], r_junk, st1[:, 0:1], r_st1)
                    rms_rstd(None, st1[:, 0:1], D, st3[:, 0:1], r_st1, r_st2, r_st3, st2[:, 0:1])
                    xn, r_xn = xn_ring.next()
                    kb.op("dve", lambda: V.tensor_scalar(out=xn[:], in0=xt[:], scalar1=st3[:, 0:1], scalar2=None,
                                                         op0=ALU.mult), reads=[r_xt, r_st3], writes=[r_xn])
                    for half in range(2):
                        ph, r_ph = ph_ring.next()
                        kb.peg([lambda c=c: P.transpose(ph[:, c * 128:(c + 1) * 128],
                                                        xn[:, (half * 8 + c) * 128:(half * 8 + c + 1) * 128],
                                                        ident_b[:]) for c in range(8)],
                               reads=[r_xn, r_ident_b], writes=[r_ph])
                        for c in range(8):
                            cc = half * 8 + c
                            if False:
                                kb.op("act", lambda c=c, cc=cc: A.activation(
                                    out=h2s[:, cc, ti * 128:(ti + 1) * 128], in_=ph[:, c * 128:(c + 1) * 128],
                                    func=AF.Identity, scale=GfT[:, cc, jm:jm + 1], bias=mT[:, 48 + cc, jm:jm + 1]),
                                    reads=[r_ph, r_Gf, r_mT], pwrites=[r_h2s])
                            else:
                                kb.op("dve", lambda c=c, cc=cc: V.tensor_scalar(
                                    out=h2s[:, cc, ti * 128:(ti + 1) * 128], in0=ph[:, c * 128:(c + 1) * 128],
                                    scalar1=GfT[:, cc, jm:jm + 1], scalar2=mT[:, 48 + cc, jm:jm + 1],
                                    op0=ALU.mult, op1=ALU.add),
                                    reads=[r_ph, r_Gf, r_mT], pwrites=[r_h2s])
                kb.dma("sp", H2T[:, :, tok0:tok0 + nb].rearrange("c p t -> p c t"), h2s[:, :, 0:nb], ds_h2s,
                       reads=[r_h2s], pwrites=[r_H2T])
            kb.barrier()

        if stop < 4:
            continue
        with ExitStack() as es:
            h2, r_h2 = sb(es, "h2", [128, DC, 512], BF16)
            ds_h2 = kb.dsem("h2")
            aT, r_aT = sb(es, "aT", [128, FC, 512], BF16)
            gu_ring = Ring([sb(es, f"gu{i}", [128, 2, DC, 512], BF16) + (kb.dsem(f"gu{i}"),) for i in range(2)])
            wd_ring = Ring([sb(es, f"wd{i}", [128, 1024], BF16) + (kb.dsem(f"wd{i}"),) for i in range(6)])
            xs, r_xs = sb(es, "xs", [128, 8, 512], F32)
            ds_xs = kb.dsem("xsl")
            ds_xo = kb.dsem("xso")
            sg_ring = Ring([sb(es, f"sg{i}", [128, 512], F32) for i in range(2)])
            tmp, r_tmp = sb(es, "ftmp", [128, 512], F32)
            gl, r_gl = sb(es, "fgl", [128, D], F32)
            gc, r_gc = sb(es, "fgc", [128, D], F32)
            banks = [pst(es, f"fb{i}", [128, 512], F32) for i in range(8)]
            ds_g = kb.dsem("f_gates")
            kb.dma("sp", gl[:], GT[l, 1, 0].unsqueeze(0).broadcast_to([128, D]), ds_g, reads=[r_GT], writes=[r_gl])
            kb.dma("sp", gc[:], GT[l, 1, 1].unsqueeze(0).broadcast_to([128, D]), ds_g, reads=[r_GT], writes=[r_gc])
            wgv = ffn_wg[l].rearrange("(k p) n -> p k n", p=128)
            wuv = ffn_wu[l].rearrange("(k p) n -> p k n", p=128)
            gu_bank = Ring([(banks[0], banks[1]), (banks[2], banks[3])])

            for (t0, nt) in cblocks:
                is_ctx = (t0 == 0)
                gate = gc if is_ctx else gl
                r_gate = r_gc if is_ctx else r_gl
                nb = nt * 128
                tok0 = t0 * 128
                kb.dma("sp", h2[:, :, 0:nb], H2T[:, :, tok0:tok0 + nb].rearrange("c p t -> p c t"), ds_h2,
                       reads=[r_H2T], writes=[r_h2])
                for gg in range(FC // 4):
                    gu, r_gu, ds_gu = gu_ring.next()
                    kb.dma("pool", gu[:, 0, :, :], wgv[:, :, gg * 512:(gg + 1) * 512], ds_gu, writes=[r_gu])
                    kb.dma("pool", gu[:, 1, :, :], wuv[:, :, gg * 512:(gg + 1) * 512], kb.dsem(ds_gu.name + "u"),
                           pwrites=[r_gu])
                    for c4 in range(4):
                        jf = gg * 4 + c4
                        (bg, r_bg), (bu, r_bu) = gu_bank.next()
                        kb.peg([lambda k=k: P.matmul(bg[:, 0:nb], gu[:, 0, k, c4 * 128:(c4 + 1) * 128], h2[:, k, 0:nb],
                                                     start=(k == 0), stop=(k == DC - 1)) for k in range(DC)] +
                               [lambda k=k: P.matmul(bu[:, 0:nb], gu[:, 1, k, c4 * 128:(c4 + 1) * 128], h2[:, k, 0:nb],
                                                     start=(k == 0), stop=(k == DC - 1)) for k in range(DC)],
                               reads=[r_gu, r_h2], writes=[r_bg, r_bu])
                        sg, r_sg = sg_ring.next()
                        kb.op("act", lambda: A.activation(out=sg[:, 0:nb], in_=bg[:, 0:nb], func=AF.Silu),
                              reads=[r_bg], writes=[r_sg])
                        kb.op("dve", lambda: V.tensor_tensor(out=aT[:, jf, 0:nb], in0=bu[:, 0:nb], in1=sg[:, 0:nb],
                                                             op=ALU.mult),
                              reads=[r_bu, r_sg], pwrites=[r_aT])
                for pss_ in range(2):
                    for ti in range(nt):
                        for c2 in range(2):
                            kb.dma("sp", xs[:, ti * 2 + c2, :],
                                   xres[(t0 + ti) * 128:(t0 + ti + 1) * 128, pss_ * 1024 + c2 * 512:pss_ * 1024 + (c2 + 1) * 512],
                                   kb.dsem(f"xsl{ti * 2 + c2}"), reads=[r_xres],
                                   **(dict(writes=[r_xs]) if (ti == 0 and c2 == 0) else dict(pwrites=[r_xs])))
                    for jf in range(FC):
                        wd, r_wd, ds_wd = wd_ring.next()
                        kb.dma("pool", wd[:], ffn_wd[l, jf * 128:(jf + 1) * 128, pss_ * 1024:(pss_ + 1) * 1024], ds_wd,
                               writes=[r_wd])
                        fns = []
                        for ti in range(nt):
                            for c2 in range(2):
                                fns.append(lambda ti=ti, c2=c2: P.matmul(
                                    banks[ti * 2 + c2][0][:], aT[:, jf, ti * 128:(ti + 1) * 128],
                                    wd[:, c2 * 512:(c2 + 1) * 512], start=(jf == 0), stop=(jf == FC - 1)))
                        kb.peg(fns, reads=[r_aT, r_wd], writes=[banks[i][1] for i in range(nt * 2)])
                    for ti in range(nt):
                        for c2 in range(2):
                            b_, r_b = banks[ti * 2 + c2]
                            col = pss_ * 1024 + c2 * 512
                            kb.op("dve", lambda: V.tensor_tensor(out=tmp[:], in0=b_[:], in1=gate[:, col:col + 512],
                                                                 op=ALU.mult), reads=[r_b, r_gate], writes=[r_tmp])
                            kb.op("dve", lambda: V.tensor_tensor(out=xs[:, ti * 2 + c2, :], in0=xs[:, ti * 2 + c2, :],
                                                                 in1=tmp[:], op=ALU.add),
                                  reads=[r_tmp, r_xs], pwrites=[r_xs])
                            kb.dma("sp", xres[(t0 + ti) * 128:(t0 + ti + 1) * 128, col:col + 512], xs[:, ti * 2 + c2, :],
                                   kb.dsem(f"xso{ti * 2 + c2}"), reads=[r_xs], pwrites=[r_xres])
            kb.barrier()

    with ExitStack() as es:
        xt_ring = Ring([sb(es, f"dxt{i}", [128, D], F32) + (kb.dsem(f"dxt{i}"), kb.dsem(f"dxo{i}")) for i in range(3)])
        junk, r_junk = sb(es, "djunk", [128, D], BF16)
        gfin, r_gfin = sb(es, "gfin", [128, D], F32)
        st1, r_st1 = sb(es, "dst1", [128, 8], F32)
        st2, r_st2 = sb(es, "dst2", [128, 8], F32)
        st3, r_st3 = sb(es, "dst3", [128, 8], F32)
        ds_gf = kb.dsem("gfin")
        kb.dma("sp", gfin[:], g_final.broadcast_to([128, D]), ds_gf, writes=[r_gfin])
        xfin = xres if n_layers > 0 else xin
        for ti in range(S // 128):
            tg = 2 + ti
            xt, r_xt, ds_xt, ds_xo = xt_ring.next()
            kb.dma("sp", xt[:], xfin[tg * 128:(tg + 1) * 128, :], ds_xt, reads=[r_xres], writes=[r_xt])
            sumsq(xt[:], r_xt, 1, junk[:], r_junk, st1[:, 0:1], r_st1)
            rms_rstd(None, st1[:, 0:1], D, st3[:, 0:1], r_st1, r_st2, r_st3, st2[:, 0:1])
            kb.op("dve", lambda: V.scalar_tensor_tensor(out=xt[:], in0=xt[:], scalar=st3[:, 0:1], in1=gfin[:],
                                                        op0=ALU.mult, op1=ALU.mult),
                  reads=[r_xt, r_st3, r_gfin], writes=[r_xt])
            kb.dma("sp", out_d[ti * 128:(ti + 1) * 128, :], xt[:], ds_xo, reads=[r_xt], pwrites=[r_out])
        kb.barrier()
    kb.es.close()
    return nc


def _rope_table(S):
    t = np.arange(S, dtype=np.int32)
    row = (t // 64).astype(np.float32)
    col = (t % 64).astype(np.float32)
    tabs = []
    for dim in (128, 64):
        d_axis = dim // 2
        inv = (1.0 / (np.float32(10000.0) ** (np.arange(0, d_axis, 2, dtype=np.float32) / np.float32(d_axis)))).astype(np.float32)
        ang = np.concatenate([row[:, None] * inv, col[:, None] * inv], axis=-1).astype(np.float32)
        c, s = np.cos(ang).astype(np.float32), np.sin(ang).astype(np.float32)
        tabs.append(np.concatenate([c, c], -1))
        tabs.append(np.concatenate([-s, s], -1))
    lat = np.concatenate(tabs, -1)
    ctx = np.zeros((CTX, 384), np.float32)
    ctx[:, 0:128] = 1.0
    ctx[:, 256:320] = 1.0
    return np.ascontiguousarray(np.concatenate([ctx, lat], 0))


def _na_bias(rpb, S):
    rows = S // 64
    nblk = S // 512
    L = rpb.shape[0]
    out = np.full((L, 4, 3, 8, 128, 512), NEG, np.float32)
    a = np.arange(128) // 64
    ck = np.arange(128) % 64
    b = np.arange(512) // 64
    cq = np.arange(512) % 64
    for pat, g in enumerate((0, 1 if nblk > 2 else 0, nblk - 1)):
        base = min(max(8 * g - 4, 0), rows - 16)
        rq = 8 * g + b
        r_start = np.clip(rq - 4, 0, rows - 8)
        c_start = np.clip(cq - 8, 0, 64 - 16)
        for j in range(8):
            rk = base + 2 * j + a
            valid = ((rk[:, None] >= r_start[None, :]) & (rk[:, None] < r_start[None, :] + 8) &
                     (ck[:, None] >= c_start[None, :]) & (ck[:, None] < c_start[None, :] + 16))
            dr = np.clip(rk[:, None] - rq[None, :] + 7, 0, 14)
            dc = np.clip(ck[:, None] - cq[None, :] + 15, 0, 30)
            vals = rpb[:, :, dr, dc]
            out[:, :, pat, j] = np.where(valid[None, None], vals, np.float32(NEG))
    return out


_PROG_CACHE = {}


def kernel(x, c, ctx, c_ctx, w_ada, b_ada, g_attn, g_ffn, w_in, w_out, na_rpb, gqa_gq, gqa_gk,
           mla_gq, mla_gkv, mla_wuq, mla_wukv, diff_lq1, diff_lk1, diff_lq2, diff_lk2, diff_gsub,
           ffn_wg, ffn_wu, ffn_wd, g_final, _debug=False, _n_layers=2, _stop=9):
    f = lambda a: np.ascontiguousarray(np.asarray(a, dtype=np.float32))
    x, c, ctx, c_ctx = f(x), f(c), f(ctx), f(c_ctx)
    B, S, _ = x.shape
    L = 2
    key = (S, _n_layers, _debug, _stop)
    if key not in _PROG_CACHE:
        _PROG_CACHE[key] = build_program(S, _n_layers, _debug, _stop)
    nc = _PROG_CACHE[key]

    shared = {
        "w_ada": f(w_ada),
        "b_adaT": f(np.transpose(f(b_ada).reshape(L, 96, 128), (0, 2, 1))),
        "g_attnT": f(np.transpose(f(g_attn).reshape(L, DC, 128), (0, 2, 1))),
        "g_ffnT": f(np.transpose(f(g_ffn).reshape(L, DC, 128), (0, 2, 1))),
        "g_final": f(g_final).reshape(1, D),
        "w_in": f(w_in), "w_out": f(w_out), "ffn_wg": f(ffn_wg), "ffn_wu": f(ffn_wu), "ffn_wd": f(ffn_wd),
        "nab": _na_bias(f(na_rpb), S),
        "gqa_g": f(np.stack([f(gqa_gq), f(gqa_gk)], axis=1)),
        "mla_gqT": f(np.transpose(f(mla_gq).reshape(L, 4, 128), (0, 2, 1))),
        "mla_gkvT": f(np.transpose(f(mla_gkv).reshape(L, 2, 128), (0, 2, 1))),
        "mla_wuq": f(mla_wuq), "mla_wukv": f(mla_wukv),
        "diff_l": f(np.stack([f(diff_lq1), f(diff_lk1), f(diff_lq2), f(diff_lk2)], axis=1)),
        "diff_gsubT": f(f(diff_gsub).T),
        "rope": _rope_table(S),
        "ident": np.eye(128, dtype=np.float32),
    }
    in_maps = []
    for b in range(B):
        m = {k: v for k, v in shared.items() if k in nc._declared_inputs}
        m["xin"] = f(np.concatenate([ctx[b], x[b]], axis=0))
        cc = np.stack([c[b].reshape(DC, 128).T, c_ctx.reshape(DC, 128).T], axis=-1)
        m["cT"] = f(cc)
        in_maps.append(m)
    res = run_bass_kernel_spmd(nc, in_maps, core_ids=list(range(B)))
    if _debug:
        return res
    return np.stack([np.asarray(res.results[b]["out"], dtype=np.float32) for b in range(B)], axis=0)
```
